# Optimizing a Trainium2 kernel written in Bass

```python
import jax, jax.numpy as jnp
from jax import lax
import numpy as np

D_MODEL = 2048
BATCH = 1
SEQ = 8192
DEPTH = 2

GRID_W = 64
GM_GROUPS = 8
GM_HEAD = 128
GM_WIDTH = GM_GROUPS * GM_HEAD
CHUNK = 128
NA_HEADS = 16
NA_HEAD_DIM = 64
NA_WIDTH = NA_HEADS * NA_HEAD_DIM
NA_KR = 8
NA_KC = 16
N_BRANCH = 2
IN_COLS = 2 * GM_WIDTH + 3 * NA_WIDTH + N_BRANCH * D_MODEL
D_FF = 5504
CONV_W = 3
RMS_EPS = 1e-6
LN_EPS = 1e-5
NEG_INF = -1e30

kernel_name = "hybrid_gmlp_natten_convffn_encoder"


def rms_norm(x, g):
    xf = x.astype(jnp.float32)
    y = xf * lax.rsqrt(jnp.mean(xf * xf, axis=-1, keepdims=True) + RMS_EPS)
    return (y * g.astype(jnp.float32)).astype(x.dtype)


def gmlp_branch(uv, ln_g, ln_b, w_s, b_s):
    b, t, _ = uv.shape
    uv = jax.nn.gelu(uv)
    u, v = jnp.split(uv, 2, axis=-1)
    vf = v.astype(jnp.float32)
    mu = jnp.mean(vf, axis=-1, keepdims=True)
    var = jnp.mean(jnp.square(vf - mu), axis=-1, keepdims=True)
    v = ((vf - mu) * lax.rsqrt(var + LN_EPS) * ln_g.astype(jnp.float32)
         + ln_b.astype(jnp.float32)).astype(u.dtype)
    v = v.reshape(b, t // CHUNK, CHUNK, GM_GROUPS, GM_HEAD)
    mixed = jnp.einsum('gpq,bnqgc->bnpgc', w_s, v) + b_s.T[None, None, :, :, None]
    return u * mixed.reshape(b, t, GM_WIDTH)


def neighbourhood_attn(q, k, v, rpb):
    b, t, _ = q.shape
    rows = t // GRID_W
    kr = min(NA_KR, rows)

    def to_grid(z):
        return z.reshape(b, rows, GRID_W, NA_HEADS, NA_HEAD_DIM).transpose(0, 3, 1, 2, 4)

    qg, kg, vg = to_grid(q), to_grid(k), to_grid(v)
    r = jnp.arange(rows)
    row_start = jnp.clip(r - kr // 2, 0, rows - kr)
    row_idx = row_start[:, None] + jnp.arange(kr)[None, :]
    k_blk = kg[:, :, row_idx]
    v_blk = vg[:, :, row_idx]

    c = jnp.arange(GRID_W)
    col_start = jnp.clip(c - NA_KC // 2, 0, GRID_W - NA_KC)
    in_win = (c[None, :] >= col_start[:, None]) & (c[None, :] < col_start[:, None] + NA_KC)
    dr_i = (row_idx - r[:, None]) + (NA_KR - 1)
    dc_i = jnp.clip(c[None, :] - c[:, None], -(NA_KC - 1), NA_KC - 1) + (NA_KC - 1)
    bias = rpb.astype(jnp.float32)[:, dr_i[:, None, :, None], dc_i[None, :, None, :]]
    bias = jnp.where(in_win[None, None, :, None, :], bias, NEG_INF)

    scale = NA_HEAD_DIM ** -0.5
    s = jnp.einsum('bhrcd,bhrikd->bhrcik', qg.astype(jnp.float32),
                   k_blk.astype(jnp.float32)) * scale + bias[None]
    p = jax.nn.softmax(s.reshape(b, NA_HEADS, rows, GRID_W, kr * GRID_W), axis=-1)
    p = p.reshape(b, NA_HEADS, rows, GRID_W, kr, GRID_W)
    o = jnp.einsum('bhrcik,bhrikd->bhrcd', p, v_blk.astype(jnp.float32))
    return o.transpose(0, 2, 3, 1, 4).reshape(b, t, NA_WIDTH).astype(q.dtype)


def mixer_block(x, g_norm, w_in, ln_g, ln_b, w_s, b_s, rpb, w_branch, w_out):
    h = rms_norm(x, g_norm)
    proj = h @ w_in
    uv, qkv, gates = jnp.split(proj, [2 * GM_WIDTH, 2 * GM_WIDTH + 3 * NA_WIDTH], axis=-1)
    q, k, v = jnp.split(qkv, 3, axis=-1)
    y_a = gmlp_branch(uv, ln_g, ln_b, w_s, b_s)
    y_b = neighbourhood_attn(q, k, v, rpb)
    g_a, g_b = jnp.split(jax.nn.sigmoid(gates), 2, axis=-1)
    merged = g_a * (y_a @ w_branch[:GM_WIDTH]) + g_b * (y_b @ w_branch[GM_WIDTH:])
    return x + merged @ w_out


def conv_ffn(x, g_norm, w_up, conv_k, conv_b, w_down):
    h = rms_norm(x, g_norm)
    up = h @ w_up
    t = up.shape[1]
    pad = jnp.pad(up, ((0, 0), (1, 1), (0, 0)))
    up = (pad[:, :t] * conv_k[0] + pad[:, 1:t + 1] * conv_k[1]
          + pad[:, 2:t + 2] * conv_k[2] + conv_b)
    gate, val = jnp.split(up, 2, axis=-1)
    return x + (jax.nn.silu(gate) * val) @ w_down


def setup_inputs(seed: int = 0) -> dict:
    key = jax.random.key(seed)
    ks = jax.random.split(key, 17)
    f32 = jnp.float32
    n = lambda k, s: jax.random.normal(k, s, f32)
    return {
        "x": n(ks[0], (BATCH, SEQ, D_MODEL)),
        "norm_mix": 1.0 + 0.1 * n(ks[1], (DEPTH, D_MODEL)),
        "w_in": n(ks[2], (DEPTH, D_MODEL, IN_COLS)) * D_MODEL ** -0.5,
        "gm_ln_g": 1.0 + 0.1 * n(ks[3], (DEPTH, GM_WIDTH)),
        "gm_ln_b": 0.01 * n(ks[4], (DEPTH, GM_WIDTH)),
        "gm_w_s": n(ks[5], (DEPTH, GM_GROUPS, CHUNK, CHUNK)) * CHUNK ** -0.5,
        "gm_b_s": 1.0 + 0.1 * n(ks[6], (DEPTH, GM_GROUPS, CHUNK)),
        "na_rpb": 0.5 * n(ks[7], (DEPTH, NA_HEADS, 2 * NA_KR - 1, 2 * NA_KC - 1)),
        "w_branch": n(ks[8], (DEPTH, GM_WIDTH + NA_WIDTH, D_MODEL)) * GM_WIDTH ** -0.5,
        "w_out": n(ks[9], (DEPTH, D_MODEL, D_MODEL)) * D_MODEL ** -0.5,
        "norm_ffn": 1.0 + 0.1 * n(ks[10], (DEPTH, D_MODEL)),
        "w_up": n(ks[11], (DEPTH, D_MODEL, 2 * D_FF)) * D_MODEL ** -0.5,
        "conv_k": n(ks[12], (DEPTH, CONV_W, 2 * D_FF)) * CONV_W ** -0.5,
        "conv_b": 0.01 * n(ks[13], (DEPTH, 2 * D_FF)),
        "w_down": n(ks[14], (DEPTH, D_FF, D_MODEL)) * D_FF ** -0.5,
        "norm_final": 1.0 + 0.1 * n(ks[15], (D_MODEL,)),
    }


def reference(x, norm_mix, w_in, gm_ln_g, gm_ln_b, gm_w_s, gm_b_s, na_rpb, w_branch, w_out,
              norm_ffn, w_up, conv_k, conv_b, w_down, norm_final):
    for l in range(DEPTH):
        x = mixer_block(x, norm_mix[l], w_in[l], gm_ln_g[l], gm_ln_b[l], gm_w_s[l], gm_b_s[l],
                        na_rpb[l], w_branch[l], w_out[l])
        x = conv_ffn(x, norm_ffn[l], w_up[l], conv_k[l], conv_b[l], w_down[l])
    return rms_norm(x, norm_final)
```

```python
import numpy as np
from contextlib import ExitStack

import concourse.bass as bass
import concourse.mybir as mybir
from concourse.bass_utils import run_bass_kernel_spmd

F32 = mybir.dt.float32
BF16 = mybir.dt.bfloat16
AF = mybir.ActivationFunctionType
ALU = mybir.AluOpType

NCORES = 8
D = 2048
NFC = 16
T = 1024
GW = 64
NSLOT = 23
EXT = NSLOT * GW
OFF = 4 * GW
DFF = 5504
NDC = 43
FEXT = T + 2
RMS_EPS = 1e-6
LN_EPS = 1e-5
NEG = -1e30
ARENA_WORDS = 46500
NRING = 6
BLK_CH = 11


class U:
    __slots__ = ("w", "r", "name")

    def __init__(self, name=""):
        self.w = None
        self.r = {}
        self.name = name


class Prog:
    ENGS = ("pe", "act", "dve", "pool", "sp")

    def __init__(self, nc, es):
        self.nc = nc
        self.es = es
        self.ops = {e: [] for e in self.ENGS}
        self.cnt = {e: 0 for e in self.ENGS}
        self.sem = {e: es.enter_context(nc.semaphore("c_" + e)) for e in self.ENGS}
        self.waited = {e: {} for e in self.ENGS}
        self.nsem = 0
        self.out_tokens = []
        self.all_sems = [self.sem[e] for e in self.ENGS]

    def new_sem(self, name):
        self.nsem += 1
        h = self.es.enter_context(self.nc.semaphore(f"{name}_{self.nsem}"))
        self.all_sems.append(h)
        return [h, 0, "dma%d" % self.nsem]

    def _wait(self, eng, tok, raw):
        key, h, val, peng = tok
        if peng == eng:
            if eng == "pe" or not raw:
                return
        w = self.waited[eng]
        if w.get(key, 0) >= val:
            return
        w[key] = val
        self.ops[eng].append(lambda e, h=h, val=val: e.wait_ge(h, val))

    def _deps(self, eng, reads, writes):
        for u in reads:
            if u.w is not None:
                self._wait(eng, u.w, True)
        for u in writes:
            if u.w is not None:
                self._wait(eng, u.w, False)
            for t in u.r.values():
                self._wait(eng, t, False)

    def op(self, eng, fn, reads=(), writes=(), signal=True):
        self._deps(eng, reads, writes)
        h = self.sem[eng]
        if signal:
            self.cnt[eng] += 1
            val = self.cnt[eng]
            self.ops[eng].append(lambda e, fn=fn, h=h: fn(e).then_inc(h, 1))
        else:
            val = self.cnt[eng] + 1
            self.ops[eng].append(lambda e, fn=fn: fn(e))
        tok = (eng, h, val, eng)
        for u in reads:
            u.r[eng] = tok
        for u in writes:
            u.w = tok
            u.r = {}
        return tok

    def dma(self, q, out_ap, in_ap, semv, reads=(), writes=(), is_output=False):
        self._deps(q, reads, writes)
        semv[1] += 16
        h, val = semv[0], semv[1]
        self.ops[q].append(lambda e, o=out_ap, i=in_ap, h=h: e.dma_start(out=o, in_=i).then_inc(h, 16))
        tok = (semv[2], h, val, None)
        for u in reads:
            u.r[semv[2]] = tok
        for u in writes:
            u.w = tok
            u.r = {}
        if is_output:
            self.out_tokens.append(tok)
        return tok

    def barrier(self):
        toks = [(e, self.sem[e], self.cnt[e], e) for e in ("pe", "act", "dve", "pool") if self.cnt[e] > 0]
        for e in ("pe", "act", "dve", "pool", "sp"):
            for t in toks:
                if t[3] != e:
                    self._wait(e, t, True)

    def finish(self):
        for t in self.out_tokens:
            self._wait("sp", t, True)

    def replay(self):
        nc = self.nc
        with nc.Block() as b0:
            @b0.sync
            def _(e):
                for h in self.all_sems:
                    e.sem_clear(h)
        with nc.Block() as block:
            @block.tensor
            def _(e):
                for f in self.ops["pe"]:
                    f(e)

            @block.scalar
            def _(e):
                for f in self.ops["act"]:
                    f(e)

            @block.vector
            def _(e):
                for f in self.ops["dve"]:
                    f(e)

            @block.gpsimd
            def _(e):
                for f in self.ops["pool"]:
                    f(e)

            @block.sync
            def _(e):
                for f in self.ops["sp"]:
                    f(e)


class Arena:
    def __init__(self, ap, words):
        self.ap = ap
        self.words = words
        self.off = 0

    def mark(self):
        return self.off

    def release(self, m):
        self.off = m

    def f32(self, n):
        a = self.ap[:, self.off:self.off + n]
        self.off += n
        assert self.off <= self.words, f"arena overflow {self.off}"
        return a

    def bf16(self, n):
        w = (n + 1) // 2
        a = self.ap[:, self.off:self.off + w].bitcast(BF16)
        self.off += w
        assert self.off <= self.words, f"arena overflow {self.off}"
        return a[:, 0:n]


class Ring:
    def __init__(self, P, arena, n):
        self.P = P
        self.n = n
        self.slots = [arena.bf16(2048) for _ in range(n)]
        self.units = [U(f"ring{i}") for i in range(n)]
        self.sems = [P.new_sem("ring") for _ in range(n)]
        self.i = 0

    def load(self, src_ap, ncols=2048):
        k = self.i % self.n
        self.i += 1
        dst = self.slots[k][:, 0:ncols]
        self.P.dma("pool", dst, src_ap, self.sems[k], writes=[self.units[k]])
        return self.slots[k], self.units[k]


def mm(P, out, lhsT, rhs, start, stop, reads, writes, signal=None):
    if signal is None:
        signal = stop
    return P.op("pe", lambda e: e.matmul(out, lhsT=lhsT, rhs=rhs, start=start, stop=stop),
                reads=reads, writes=writes, signal=signal)


def rmsnorm_stats(P, ps, banks_u, ones, ncols, get_x, x_units, sq_bufs, sq_units, rstd, rstd_u):
    groups = []
    c = 0
    while c < ncols:
        n = min(512, ncols - c)
        groups.append((c, n))
        c += n
    for fc in range(NFC):
        xa = get_x(fc)
        sq = sq_bufs[fc % len(sq_bufs)]
        squ = sq_units[fc % len(sq_bufs)]
        P.op("act", lambda e, o=sq[:, 0:ncols], i=xa: e.activation(out=o, in_=i, func=AF.Square),
             reads=[x_units[fc]], writes=[squ])
        for gi, (c0, n) in enumerate(groups):
            mm(P, ps[:, gi * 512:gi * 512 + n], ones[:, 0:128], sq[:, c0:c0 + n],
               start=(fc == 0), stop=(fc == NFC - 1), reads=[squ], writes=[banks_u[gi]], signal=True)
    for gi, (c0, n) in enumerate(groups):
        P.op("act", lambda e, o=rstd[:, c0:c0 + n], i=ps[:, gi * 512:gi * 512 + n]:
             e.activation(out=o, in_=i, func=AF.Sqrt, bias=RMS_EPS_AP[0], scale=1.0 / D),
             reads=[banks_u[gi]], writes=[rstd_u])
    P.op("dve", lambda e, o=rstd[:, 0:ncols]: e.reciprocal(out=o, in_=o), reads=[rstd_u], writes=[rstd_u])


RMS_EPS_AP = [None]
LN_EPS_AP = [None]


def setup_common(nc, es, P):
    arena_t = es.enter_context(nc.sbuf_tensor("arena", [128, ARENA_WORDS], F32))
    ps_t = es.enter_context(nc.psum_tensor("ps", [128, 4096], F32))
    A = Arena(arena_t[:, :], ARENA_WORDS)
    ps = ps_t[:, :]
    banks = [ps[:, b * 512:(b + 1) * 512] for b in range(8)]
    bank_u = [U(f"bank{b}") for b in range(8)]
    return A, ps, banks, bank_u


def make_consts(P, A):
    ones = A.bf16(128)
    zeros = A.bf16(128)
    cu = U("consts")
    epsr = A.f32(1)
    epsl = A.f32(1)
    P.op("pool", lambda e: e.memset(ones, 1.0), writes=[cu])
    P.op("pool", lambda e: e.memset(zeros, 0.0), writes=[cu])
    P.op("pool", lambda e: e.memset(epsr, RMS_EPS), writes=[cu])
    P.op("pool", lambda e: e.memset(epsl, LN_EPS), writes=[cu])
    RMS_EPS_AP[0] = epsr
    LN_EPS_AP[0] = epsl
    return ones, zeros, cu


def build_mixer():
    nc = bass.Bass("TRN2", target_bir_lowering=False)
    dt = lambda name, shape, kind="ExternalInput": nc.dram_tensor(name, shape, F32, kind=kind).ap()
    xT_d = dt("xT", [128, NFC, EXT])
    gmix_d = dt("gmix", [128, NFC])
    win_d = dt("win_t", [72, 128, NFC * 128])
    wv_d = dt("wv_t", [2, 4, 128, 2048])
    lng_d = dt("lng", [128, 1024])
    lnb_d = dt("lnb", [128, 1024])
    wsT_d = dt("wsT", [128, 8 * 128])
    bs_d = dt("bs", [128, 8 * 128])
    bias_d = dt("bias", [8, 128, 1536])
    ident_d = dt("ident", [128, 128])
    wb_d = dt("wb_t", [16, 128, NFC * 128])
    wout_d = dt("wout_t", [16, 128, NFC * 128])
    out_d = dt("out", [128, NFC, T], kind="ExternalOutput")

    with ExitStack() as es:
        P = Prog(nc, es)
        A, ps, banks, bank_u = setup_common(nc, es, P)
        ones, zeros, cu = make_consts(P, A)
        ident = A.bf16(128)
        gmix = A.f32(NFC)
        misc_sem = P.new_sem("misc")
        hT = A.bf16(NFC * EXT).rearrange("p (a b) -> p a b", b=EXT)
        hT_u = [U(f"hT{i}") for i in range(NFC)]
        ya = A.bf16(8 * T).rearrange("p (a b) -> p a b", b=T)
        ya_u = [U(f"ya{i}") for i in range(8)]
        yb = A.bf16(8 * T).rearrange("p (a b) -> p a b", b=T)
        yb_u = [U(f"yb{i}") for i in range(8)]
        ring = Ring(P, A, NRING)
        P.dma("pool", ident, ident_d, misc_sem, writes=[cu])
        s2 = P.new_sem("misc")
        P.dma("sp", gmix, gmix_d, s2, writes=[cu])
        base = A.mark()

        rstd = A.f32(EXT)
        rstd_u = U("rstd")
        xb = [A.f32(EXT) for _ in range(3)]
        xb_u = [U(f"xb{i}") for i in range(3)]
        xb_s = [P.new_sem("xb") for _ in range(3)]
        sq = [A.bf16(EXT) for _ in range(2)]
        sq_u = [U(f"sq{i}") for i in range(2)]

        class XU:
            def __getitem__(self, fc):
                return xb_u[fc % 3]

        def get_x(fc):
            P.dma("sp", xb[fc % 3], xT_d[:, fc, :], xb_s[fc % 3], writes=[xb_u[fc % 3]])
            return xb[fc % 3]

        rmsnorm_stats(P, ps, bank_u, ones, EXT, get_x, XU(), sq, sq_u, rstd, rstd_u)
        for fc in range(NFC):
            xa = get_x(fc)
            P.op("dve", lambda e, o=hT[:, fc, :], i=xa, g=gmix[:, fc:fc + 1]:
                 e.scalar_tensor_tensor(out=o, in0=i, scalar=g, in1=rstd, op0=ALU.mult, op1=ALU.mult),
                 reads=[xb_u[fc % 3], rstd_u, cu], writes=[hT_u[fc]])
        P.barrier()
        A.release(base)

        vn = yb
        vn_u = [U(f"vn{i}") for i in range(8)]
        vg = A.f32(8 * 1024).rearrange("p (a b) -> p a b", b=1024)
        vg_u = [U(f"vg{i}") for i in range(8)]
        lng = A.f32(1024)
        lnb = A.f32(1024)
        wsT = A.bf16(1024).rearrange("p (a b) -> p a b", b=128)
        bs = A.f32(1024).rearrange("p (a b) -> p a b", b=128)
        ug = [A.f32(T) for _ in range(2)]
        ug_u = [U("ug0"), U("ug1")]
        tmp = [A.f32(T) for _ in range(2)]
        tmp_u = [U("tmp0"), U("tmp1")]
        junk = A.bf16(1024)
        junk_u = U("junk")
        s1 = A.f32(16)
        s2v = A.f32(8)
        mu = A.f32(8)
        var = A.f32(8)
        rs = A.f32(8)
        st_u = U("stats")
        pu = U("gparams")
        for (dst, src) in ((lng, lng_d), (lnb, lnb_d), (bs.rearrange("p a b -> p (a b)"), bs_d)):
            P.dma("sp", dst, src, P.new_sem("par"), writes=[pu])
        P.dma("pool", wsT.rearrange("p a b -> p (a b)"), wsT_d, P.new_sem("par"), writes=[pu])

        bi = 0
        for fh in range(2):
            wsl = [ring.load(wv_d[fh, g4]) for g4 in range(4)]
            for tt in range(8):
                b = 3 + (bi % 4)
                bi += 1
                for kc in range(NFC):
                    sl, su = wsl[kc // 4]
                    mm(P, banks[b], hT[:, kc, OFF + tt * 128:OFF + (tt + 1) * 128],
                       sl[:, (kc % 4) * 512:(kc % 4 + 1) * 512], start=(kc == 0), stop=(kc == NFC - 1),
                       reads=[hT_u[kc], su], writes=[bank_u[b]])
                P.op("act", lambda e, o=vg[:, tt, fh * 512:(fh + 1) * 512], i=banks[b],
                     acc=s1[:, tt * 2 + fh:tt * 2 + fh + 1]:
                     e.activation(out=o, in_=i, func=AF.Gelu_apprx_tanh, accum_out=acc),
                     reads=[bank_u[b]], writes=[vg_u[tt], st_u])
        for tt in range(8):
            P.op("act", lambda e, i=vg[:, tt, :], acc=s2v[:, tt:tt + 1]:
                 e.activation(out=junk, in_=i, func=AF.Square, accum_out=acc),
                 reads=[vg_u[tt]], writes=[junk_u, st_u])
        s1v = s1.rearrange("p (a b) -> p a b", b=2)
        P.op("dve", lambda e: e.tensor_tensor(out=mu, in0=s1v[:, :, 0], in1=s1v[:, :, 1], op=ALU.add),
             reads=[st_u], writes=[st_u])
        P.op("dve", lambda e: e.tensor_scalar(out=mu, in0=mu, scalar1=1.0 / 1024, scalar2=None, op0=ALU.mult),
             reads=[st_u], writes=[st_u])
        P.op("dve", lambda e: e.tensor_tensor(out=var, in0=mu, in1=mu, op=ALU.mult), reads=[st_u], writes=[st_u])
        P.op("dve", lambda e: e.scalar_tensor_tensor(out=var, in0=s2v, scalar=1.0 / 1024, in1=var,
                                                     op0=ALU.mult, op1=ALU.subtract), reads=[st_u], writes=[st_u])
        P.op("act", lambda e: e.activation(out=rs, in_=var, func=AF.Sqrt, bias=LN_EPS_AP[0], scale=1.0),
             reads=[st_u, cu], writes=[st_u])
        P.op("dve", lambda e: e.reciprocal(out=rs, in_=rs), reads=[st_u], writes=[st_u])
        for tt in range(8):
            P.op("dve", lambda e, v=vg[:, tt, :], m=mu[:, tt:tt + 1], r=rs[:, tt:tt + 1]:
                 e.tensor_scalar(out=v, in0=v, scalar1=m, scalar2=r, op0=ALU.subtract, op1=ALU.mult),
                 reads=[vg_u[tt], st_u], writes=[vg_u[tt]])
            P.op("pool", lambda e, v=vg[:, tt, :]: e.tensor_tensor(out=v, in0=v, in1=lng, op=ALU.mult),
                 reads=[vg_u[tt], pu], writes=[vg_u[tt]])
            P.op("dve", lambda e, v=vg[:, tt, :], o=vn[:, tt, :]: e.tensor_tensor(out=o, in0=v, in1=lnb, op=ALU.add),
                 reads=[vg_u[tt], pu], writes=[vn_u[tt]])
        for g in range(8):
            sl, su = ring.load(win_d[g])
            slv = sl.rearrange("p (a b) -> p a b", b=128)
            ub = (0, 1) if g % 2 == 0 else (2, 3)
            mb = (4, 5) if g % 2 == 0 else (6, 7)
            for th in range(2):
                b = ub[th]
                for kc in range(NFC):
                    mm(P, banks[b], slv[:, kc, :], hT[:, kc, OFF + th * 512:OFF + (th + 1) * 512],
                       start=(kc == 0), stop=(kc == NFC - 1), reads=[hT_u[kc], su], writes=[bank_u[b]])
                P.op("act", lambda e, o=ug[g % 2][:, th * 512:(th + 1) * 512], i=banks[b]:
                     e.activation(out=o, in_=i, func=AF.Gelu_apprx_tanh),
                     reads=[bank_u[b]], writes=[ug_u[g % 2]])
            for tt in range(8):
                b = mb[tt // 4]
                mm(P, banks[b][:, (tt % 4) * 128:(tt % 4 + 1) * 128], vn[:, tt, g * 128:(g + 1) * 128], wsT[:, g, :],
                   start=True, stop=True, reads=[vn_u[tt], pu], writes=[bank_u[b]])
            psm = ps[:, mb[0] * 512:mb[0] * 512 + 1024].rearrange("p (a b) -> p a b", b=128)
            bsg = bs[:, g, :]
            bsb = bass.AP(bsg.tensor, bsg.offset, [bsg.ap[0], [0, 8], bsg.ap[-1]])
            P.op("dve", lambda e, o=tmp[g % 2].rearrange("p (a b) -> p a b", b=128), i=psm, b2=bsb:
                 e.tensor_tensor(out=o, in0=i, in1=b2, op=ALU.add),
                 reads=[bank_u[mb[0]], bank_u[mb[1]], pu], writes=[tmp_u[g % 2]])
            P.op("dve", lambda e, o=ya[:, g, :], a=tmp[g % 2], b2=ug[g % 2]:
                 e.tensor_tensor(out=o, in0=a, in1=b2, op=ALU.mult),
                 reads=[tmp_u[g % 2], ug_u[g % 2]], writes=[ya_u[g]])
        P.barrier()
        A.release(base)

        QT = [A.bf16(T) for _ in range(2)]
        KT = [A.bf16(EXT) for _ in range(2)]
        VT = [A.bf16(EXT) for _ in range(2)]
        V2 = [A.bf16(EXT).rearrange("p (a b) -> p a b", b=64) for _ in range(2)]
        QT_u = [U("QT0"), U("QT1")]
        KT_u = [U("KT0"), U("KT1")]
        VT_u = [U("VT0"), U("VT1")]
        V2_u = [U("V20"), U("V21")]
        biasb = [A.f32(1536) for _ in range(2)]
        bias_u = [U("bias0"), U("bias1")]
        bias_s = [P.new_sem("bias") for _ in range(2)]
        Ssb = [A.f32(512) for _ in range(2)]
        Ssb_u = [U("Ssb0"), U("Ssb1")]
        Eb = [A.bf16(512) for _ in range(3)]
        E_u = [U("E0"), U("E1"), U("E2")]
        rec = A.f32(T)
        rec_u = U("rec")
        SPEC_OFF = {0: 512, 1: 576, 2: 704, 3: 896, 20: 1152, 21: 1344, 22: 1472}
        ext_groups = [(0, 512), (512, 512), (1024, 448)]
        pbi = 0
        for j in range(8):
            jb = j % 2
            P.dma("sp", biasb[jb], bias_d[j], bias_s[jb], writes=[bias_u[jb]])
            for (which, chunk) in (("q", 16 + j), ("k", 24 + j), ("v", 32 + j)):
                sl, su = ring.load(win_d[chunk])
                slv = sl.rearrange("p (a b) -> p a b", b=128)
                grps = [(OFF, 512), (OFF + 512, 512)] if which == "q" else ext_groups
                for (c0, n) in grps:
                    b = 6 + (pbi % 2)
                    pbi += 1
                    for kc in range(NFC):
                        mm(P, banks[b][:, 0:n], slv[:, kc, :], hT[:, kc, c0:c0 + n],
                           start=(kc == 0), stop=(kc == NFC - 1), reads=[hT_u[kc], su], writes=[bank_u[b]])
                    if which == "q":
                        P.op("act", lambda e, o=QT[jb][:, c0 - OFF:c0 - OFF + n], i=banks[b][:, 0:n]:
                             e.activation(out=o, in_=i, func=AF.Copy, scale=0.125),
                             reads=[bank_u[b]], writes=[QT_u[jb]])
                    elif which == "k":
                        P.op("dve", lambda e, o=KT[jb][:, c0:c0 + n], i=banks[b][:, 0:n]: e.tensor_copy(out=o, in_=i),
                             reads=[bank_u[b]], writes=[KT_u[jb]])
                    else:
                        P.op("act", lambda e, o=VT[jb][:, c0:c0 + n], i=banks[b][:, 0:n]:
                             e.activation(out=o, in_=i, func=AF.Copy),
                             reads=[bank_u[b]], writes=[VT_u[jb]])
            for s0 in (0, 8, 16):
                ns = min(8, NSLOT - s0)
                b = 6 + (pbi % 2)
                pbi += 1
                pT = banks[b].bitcast(BF16).rearrange("p (a b) -> p a b", b=128)
                for si in range(ns):
                    s = s0 + si
                    for hb in (0, 64):
                        last = (si == ns - 1 and hb == 64)
                        P.op("pe", lambda e, o=pT[hb:hb + 64, si, :], i=VT[jb][:, s * 64:(s + 1) * 64]:
                             e.transpose(o, i, ident), reads=[VT_u[jb], cu], writes=[bank_u[b]], signal=last)
                P.op("dve", lambda e, o=V2[jb][0:64, s0:s0 + ns, :], i=pT[0:64, 0:ns, 0:64]: e.tensor_copy(out=o, in_=i),
                     reads=[bank_u[b]], writes=[V2_u[jb]])
                P.op("dve", lambda e, o=V2[jb][64:128, s0:s0 + ns, :], i=pT[64:128, 0:ns, 64:128]: e.tensor_copy(out=o, in_=i),
                     reads=[bank_u[b]], writes=[V2_u[jb]])
            for b in (2, 3, 4, 5):
                mm(P, banks[b], zeros[:, 0:128], QT[jb][:, 0:512], start=True, stop=True,
                   reads=[cu, QT_u[jb]], writes=[bank_u[b]])

            def emit_qk(s):
                lo, hi = max(0, s - 7), min(15, s)
                n = (hi - lo + 1) * 64
                b = s % 2
                for hb in (0, 64):
                    mm(P, banks[b][hb:hb + 64, 0:n], KT[jb][hb:hb + 64, s * 64:(s + 1) * 64],
                       QT[jb][hb:hb + 64, lo * 64:(hi + 1) * 64], start=True, stop=True,
                       reads=[KT_u[jb], QT_u[jb]], writes=[bank_u[b]], signal=(hb == 64))
                if s in SPEC_OFF:
                    bo = SPEC_OFF[s]
                else:
                    bo = (lo - (s - 7)) * 64
                P.op("dve", lambda e, o=Ssb[s % 2][:, 0:n], i=banks[b][:, 0:n], bb=biasb[jb][:, bo:bo + n]:
                     e.tensor_tensor(out=o, in0=i, in1=bb, op=ALU.add),
                     reads=[bank_u[b], bias_u[jb]], writes=[Ssb_u[s % 2]])
                P.op("act", lambda e, o=Eb[s % 3][:, 0:n], i=Ssb[s % 2][:, 0:n]: e.activation(out=o, in_=i, func=AF.Exp),
                     reads=[Ssb_u[s % 2]], writes=[E_u[s % 3]])

            def emit_pv(s):
                lo, hi = max(0, s - 7), min(15, s)
                segs = []
                if lo < 8:
                    segs.append((lo, min(hi, 7)))
                if hi >= 8:
                    segs.append((max(lo, 8), hi))
                items = []
                for (a, bnd) in segs:
                    for hb in (0, 64):
                        for kind in ("o", "s"):
                            items.append((a, bnd, hb, kind))
                for idx, (a, bnd, hb, kind) in enumerate(items):
                    bank = (2 if kind == "o" else 4) + (a // 8)
                    c0 = (a % 8) * 64
                    n = (bnd - a + 1) * 64
                    e0 = (a - lo) * 64
                    lhs = V2[jb][hb:hb + 64, s, :] if kind == "o" else ones[hb:hb + 64, 0:64]
                    mm(P, banks[bank][hb:hb + 64, c0:c0 + n], lhs, Eb[s % 3][hb:hb + 64, e0:e0 + n],
                       start=False, stop=True, reads=[V2_u[jb], E_u[s % 3], cu], writes=[bank_u[bank]],
                       signal=(idx == len(items) - 1))

            LA = 1
            for s in range(NSLOT + LA):
                if s < NSLOT:
                    emit_qk(s)
                if s - LA >= 0:
                    emit_pv(s - LA)
            P.op("dve", lambda e: e.reciprocal(out=rec, in_=ps[:, 4 * 512:6 * 512]),
                 reads=[bank_u[4], bank_u[5]], writes=[rec_u])
            P.op("dve", lambda e, o=yb[:, j, :]: e.tensor_tensor(out=o, in0=ps[:, 2 * 512:4 * 512], in1=rec, op=ALU.mult),
                 reads=[bank_u[2], bank_u[3], rec_u], writes=[yb_u[j]])
        P.barrier()
        A.release(base)

        mg = A.bf16(NFC * T).rearrange("p (a b) -> p a b", b=T)
        mg_u = [U(f"mg{i}") for i in range(NFC)]
        sa = [A.f32(512) for _ in range(2)]
        sb = [A.f32(512) for _ in range(2)]
        t1 = [A.f32(512) for _ in range(2)]
        t2 = [A.f32(512) for _ in range(2)]
        sa_u = [U("sa0"), U("sa1")]
        sb_u = [U("sb0"), U("sb1")]
        t1_u = [U("t10"), U("t11")]
        t2_u = [U("t20"), U("t21")]
        xr = [A.f32(512) for _ in range(2)]
        xr_u = [U("xr0"), U("xr1")]
        xr_s = [P.new_sem("xr") for _ in range(2)]
        xo = [A.f32(512) for _ in range(2)]
        xo_u = [U("xo0"), U("xo1")]
        xo_s = [P.new_sem("xo") for _ in range(2)]
        it = 0
        for oc in range(NFC):
            wbs, wbu = ring.load(wb_d[oc])
            wgas, wgau = ring.load(win_d[40 + oc])
            wgbs, wgbu = ring.load(win_d[56 + oc])
            wbv = wbs.rearrange("p (a b) -> p a b", b=128)
            wgav = wgas.rearrange("p (a b) -> p a b", b=128)
            wgbv = wgbs.rearrange("p (a b) -> p a b", b=128)
            for th in range(2):
                k = it % 2
                it += 1
                bA, bB, bGa, bGb = 4 * k, 4 * k + 1, 4 * k + 2, 4 * k + 3
                tsl = slice(th * 512, (th + 1) * 512)
                hsl = slice(OFF + th * 512, OFF + (th + 1) * 512)
                for kc in range(8):
                    mm(P, banks[bA], wbv[:, kc, :], ya[:, kc, tsl], start=(kc == 0), stop=(kc == 7),
                       reads=[wbu, ya_u[kc]], writes=[bank_u[bA]])
                for kc in range(8):
                    mm(P, banks[bB], wbv[:, 8 + kc, :], yb[:, kc, tsl], start=(kc == 0), stop=(kc == 7),
                       reads=[wbu, yb_u[kc]], writes=[bank_u[bB]])
                for kc in range(NFC):
                    mm(P, banks[bGa], wgav[:, kc, :], hT[:, kc, hsl], start=(kc == 0), stop=(kc == NFC - 1),
                       reads=[wgau, hT_u[kc]], writes=[bank_u[bGa]])
                for kc in range(NFC):
                    mm(P, banks[bGb], wgbv[:, kc, :], hT[:, kc, hsl], start=(kc == 0), stop=(kc == NFC - 1),
                       reads=[wgbu, hT_u[kc]], writes=[bank_u[bGb]])
                P.op("act", lambda e, o=sa[k], i=banks[bGa]: e.activation(out=o, in_=i, func=AF.Sigmoid),
                     reads=[bank_u[bGa]], writes=[sa_u[k]])
                P.op("act", lambda e, o=sb[k], i=banks[bGb]: e.activation(out=o, in_=i, func=AF.Sigmoid),
                     reads=[bank_u[bGb]], writes=[sb_u[k]])
                P.op("dve", lambda e, o=t1[k], a=sa[k], b2=banks[bA]: e.tensor_tensor(out=o, in0=a, in1=b2, op=ALU.mult),
                     reads=[sa_u[k], bank_u[bA]], writes=[t1_u[k]])
                P.op("dve", lambda e, o=t2[k], a=sb[k], b2=banks[bB]: e.tensor_tensor(out=o, in0=a, in1=b2, op=ALU.mult),
                     reads=[sb_u[k], bank_u[bB]], writes=[t2_u[k]])
                P.op("pool", lambda e, o=mg[:, oc, tsl], a=t1[k], b2=t2[k]: e.tensor_tensor(out=o, in0=a, in1=b2, op=ALU.add),
                     reads=[t1_u[k], t2_u[k]], writes=[mg_u[oc]])
        it = 0
        for oc in range(NFC):
            wos, wou = ring.load(wout_d[oc])
            wov = wos.rearrange("p (a b) -> p a b", b=128)
            for th in range(2):
                k = it % 2
                b = it % 4
                it += 1
                tsl = slice(th * 512, (th + 1) * 512)
                P.dma("sp", xr[k], xT_d[:, oc, OFF + th * 512:OFF + (th + 1) * 512], xr_s[k], writes=[xr_u[k]])
                for kc in range(NFC):
                    mm(P, banks[b], wov[:, kc, :], mg[:, kc, tsl], start=(kc == 0), stop=(kc == NFC - 1),
                       reads=[wou, mg_u[kc]], writes=[bank_u[b]])
                P.op("dve", lambda e, o=xo[k], a=xr[k], b2=banks[b]: e.tensor_tensor(out=o, in0=a, in1=b2, op=ALU.add),
                     reads=[xr_u[k], bank_u[b]], writes=[xo_u[k]])
                P.dma("sp", out_d[:, oc, tsl], xo[k], xo_s[k], reads=[xo_u[k]], is_output=True)
        P.finish()
        P.replay()
    return nc


def build_ffn(final):
    nc = bass.Bass("TRN2", target_bir_lowering=False)
    dt = lambda name, shape, kind="ExternalInput": nc.dram_tensor(name, shape, F32, kind=kind).ap()
    xm_d = dt("xm", [128, NFC, FEXT])
    gffn_d = dt("gffn", [128, NFC])
    wup_d = dt("wup_t", [2 * NDC, 128, NFC * 128])
    ck_d = dt("ck", [128, 2 * NDC * 3])
    cb_d = dt("cb", [128, 2 * NDC])
    wd_d = dt("wd_t", [4, NFC, 128, BLK_CH * 128])
    gfin_d = dt("gfin", [128, NFC])
    out_d = dt("out", [128, NFC, T], kind="ExternalOutput")

    with ExitStack() as es:
        P = Prog(nc, es)
        A, ps, banks, bank_u = setup_common(nc, es, P)
        ones, zeros, cu = make_consts(P, A)
        gffn = A.f32(NFC)
        gfin = A.f32(NFC)
        ck = A.f32(2 * NDC * 3).rearrange("p (a b) -> p a b", b=3)
        cb = A.f32(2 * NDC)
        for (dst, src) in ((gffn, gffn_d), (gfin, gfin_d), (ck.rearrange("p a b -> p (a b)"), ck_d), (cb, cb_d)):
            P.dma("sp", dst, src, P.new_sem("par"), writes=[cu])
        xT = A.f32(NFC * FEXT).rearrange("p (a b) -> p a b", b=FEXT)
        xT_u = [U(f"xT{i}") for i in range(NFC)]
        h2 = A.bf16(NFC * FEXT).rearrange("p (a b) -> p a b", b=FEXT)
        h2_u = [U(f"h2{i}") for i in range(NFC)]
        ring = Ring(P, A, NRING)
        rstd = A.f32(FEXT)
        rstd_u = U("rstd")
        base = A.mark()

        sq = [A.bf16(FEXT) for _ in range(2)]
        sq_u = [U("sq0"), U("sq1")]
        for q4 in range(4):
            P.dma("sp", xT[:, 4 * q4:4 * q4 + 4, :], xm_d[:, 4 * q4:4 * q4 + 4, :], P.new_sem("xl"),
                  writes=xT_u[4 * q4:4 * q4 + 4])
        rmsnorm_stats(P, ps, bank_u, ones, FEXT, lambda fc: xT[:, fc, :], xT_u, sq, sq_u, rstd, rstd_u)
        for fc in range(NFC):
            P.op("dve", lambda e, o=h2[:, fc, :], i=xT[:, fc, :], g=gffn[:, fc:fc + 1]:
                 e.scalar_tensor_tensor(out=o, in0=i, scalar=g, in1=rstd, op0=ALU.mult, op1=ALU.mult),
                 reads=[xT_u[fc], rstd_u, cu], writes=[h2_u[fc]])
        P.barrier()
        A.release(base)

        act = A.bf16(BLK_CH * T).rearrange("p (a b) -> p a b", b=T)
        act_u = [U(f"act{i}") for i in range(BLK_CH)]
        upsb = [[A.f32(FEXT) for _ in range(2)] for _ in range(2)]
        upsb_u = [[U(f"up{i}{w}") for w in range(2)] for i in range(2)]
        accg = [A.f32(512) for _ in range(2)]
        accv = [A.f32(512) for _ in range(2)]
        sg = [A.f32(512) for _ in range(2)]
        accg_u = [U("ag0"), U("ag1")]
        accv_u = [U("av0"), U("av1")]
        sg_u = [U("sg0"), U("sg1")]
        NH = 8
        halo_u = [U(f"halo{i}") for i in range(NH)]
        hi_ = 0
        ubi = 0
        dbi = 0
        thi = 0
        h2halo = lambda kc: bass.AP(h2[:, kc, 0:1].tensor, h2[:, kc, 0:1].offset, [h2[:, kc, 0:1].ap[0], [FEXT - 1, 2]])
        for blk in range(4):
            c_lo = blk * BLK_CH
            nci = min(BLK_CH, NDC - c_lo)
            for ci in range(nci):
                dc = c_lo + ci
                buf = dc % 2
                for w in range(2):
                    col = dc + w * NDC
                    sl, su = ring.load(wup_d[col])
                    slv = sl.rearrange("p (a b) -> p a b", b=128)
                    dst = upsb[buf][w]
                    dst_u = upsb_u[buf][w]
                    for th in range(2):
                        b = ubi % 4
                        ubi += 1
                        for kc in range(NFC):
                            mm(P, banks[b], slv[:, kc, :], h2[:, kc, 1 + th * 512:1 + (th + 1) * 512],
                               start=(kc == 0), stop=(kc == NFC - 1), reads=[su, h2_u[kc]], writes=[bank_u[b]])
                        P.op("act", lambda e, o=dst[:, 1 + th * 512:1 + (th + 1) * 512], i=banks[b]:
                             e.activation(out=o, in_=i, func=AF.Copy), reads=[bank_u[b]], writes=[dst_u])
                    hk = hi_ % NH
                    hi_ += 1
                    hps = banks[7][:, 2 * hk:2 * hk + 2]
                    for kc in range(NFC):
                        mm(P, hps, slv[:, kc, :], h2halo(kc), start=(kc == 0), stop=(kc == NFC - 1),
                           reads=[su, h2_u[kc]], writes=[halo_u[hk]])
                    dsth = bass.AP(dst[:, 0:1].tensor, dst[:, 0:1].offset, [dst[:, 0:1].ap[0], [FEXT - 1, 2]])
                    P.op("act", lambda e, o=dsth, i=hps: e.activation(out=o, in_=i, func=AF.Copy),
                         reads=[halo_u[hk]], writes=[dst_u])
                for th in range(2):
                    k = thi % 2
                    thi += 1
                    for w, acc, acc_u in ((0, accg[k], accg_u[k]), (1, accv[k], accv_u[k])):
                        col = dc + w * NDC
                        src = upsb[buf][w]
                        src_u = upsb_u[buf][w]
                        e0 = 1 + th * 512
                        P.op("pool", lambda e, o=acc, i=src[:, e0:e0 + 512], k1=ck[:, col, 1:2], b1=cb[:, col:col + 1]:
                             e.tensor_scalar(out=o, in0=i, scalar1=k1, scalar2=b1, op0=ALU.mult, op1=ALU.add),
                             reads=[src_u, cu], writes=[acc_u])
                        P.op("dve", lambda e, o=acc, i=src[:, e0 - 1:e0 + 511], k0=ck[:, col, 0:1]:
                             e.scalar_tensor_tensor(out=o, in0=i, scalar=k0, in1=o, op0=ALU.mult, op1=ALU.add),
                             reads=[src_u, acc_u, cu], writes=[acc_u])
                        P.op("dve", lambda e, o=acc, i=src[:, e0 + 1:e0 + 513], k2=ck[:, col, 2:3]:
                             e.scalar_tensor_tensor(out=o, in0=i, scalar=k2, in1=o, op0=ALU.mult, op1=ALU.add),
                             reads=[src_u, acc_u, cu], writes=[acc_u])
                    P.op("act", lambda e, o=sg[k], i=accg[k]: e.activation(out=o, in_=i, func=AF.Silu),
                         reads=[accg_u[k]], writes=[sg_u[k]])
                    P.op("dve", lambda e, o=act[:, ci, th * 512:(th + 1) * 512], a=sg[k], b2=accv[k]:
                         e.tensor_tensor(out=o, in0=a, in1=b2, op=ALU.mult),
                         reads=[sg_u[k], accv_u[k]], writes=[act_u[ci]])
            for oc in range(NFC):
                sl, su = ring.load(wd_d[blk, oc][:, 0:nci * 128], ncols=nci * 128)
                slv = sl[:, 0:nci * 128].rearrange("p (a b) -> p a b", b=128)
                for th in range(2):
                    b = 4 + (dbi % 3)
                    dbi += 1
                    for ci in range(nci):
                        mm(P, banks[b], slv[:, ci, :], act[:, ci, th * 512:(th + 1) * 512],
                           start=(ci == 0), stop=(ci == nci - 1), reads=[su, act_u[ci]], writes=[bank_u[b]])
                    xs = xT[:, oc, 1 + th * 512:1 + (th + 1) * 512]
                    P.op("dve", lambda e, o=xs, i=banks[b]: e.tensor_tensor(out=o, in0=o, in1=i, op=ALU.add),
                         reads=[bank_u[b], xT_u[oc]], writes=[xT_u[oc]])
        P.barrier()
        A.release(base)

        if final:
            sq2 = [A.bf16(T) for _ in range(2)]
            sq2_u = [U("sqf0"), U("sqf1")]
            ob = [A.f32(T) for _ in range(2)]
            ob_u = [U("ob0"), U("ob1")]
            ob_s = [P.new_sem("ob") for _ in range(2)]
            rmsnorm_stats(P, ps, bank_u, ones, T, lambda fc: xT[:, fc, 1:1 + T], xT_u, sq2, sq2_u, rstd, rstd_u)
            for fc in range(NFC):
                k = fc % 2
                P.op("dve", lambda e, o=ob[k], i=xT[:, fc, 1:1 + T], g=gfin[:, fc:fc + 1]:
                     e.scalar_tensor_tensor(out=o, in0=i, scalar=g, in1=rstd[:, 0:T], op0=ALU.mult, op1=ALU.mult),
                     reads=[xT_u[fc], rstd_u, cu], writes=[ob_u[k]])
                P.dma("sp", out_d[:, fc, :], ob[k], ob_s[k], reads=[ob_u[k]], is_output=True)
        else:
            for q4 in range(4):
                P.dma("sp", out_d[:, 4 * q4:4 * q4 + 4, :], xT[:, 4 * q4:4 * q4 + 4, 1:1 + T], P.new_sem("xo"),
                      reads=xT_u[4 * q4:4 * q4 + 4], is_output=True)
        P.finish()
        P.replay()
    return nc


def _fm(v):
    return np.ascontiguousarray(v.reshape(NFC, 128).T)


def _tile_w(w):
    k, n = w.shape
    return np.ascontiguousarray(w.reshape(k // 128, 128, n // 128, 128).transpose(2, 1, 0, 3)).reshape(n // 128, 128, (k // 128) * 128)


def _slot_rows(c):
    rows = [c * 16 - 4 + s for s in range(NSLOT)]
    if c == 0:
        rows[0:4] = [4, 5, 6, 7]
    if c == NCORES - 1:
        rows[20:23] = [120, 121, 122]
    return rows


def _bias_tables(rpb):
    qc = np.arange(64)
    kc = np.arange(64)
    cs = np.clip(qc - 8, 0, 48)
    inwin = (kc[:, None] >= cs[None, :]) & (kc[:, None] < cs[None, :] + 16)
    dci = np.clip(kc[:, None] - qc[None, :], -15, 15) + 15
    tabs = []
    for c in range(NCORES):
        rows = _slot_rows(c)
        tab = np.full((16, 64, 1536), NEG, np.float32)

        def blockfor(s, lr):
            krg = rows[s]
            qrg = c * 16 + lr
            rs = min(max(qrg - 4, 0), 120)
            if not (rs <= krg < rs + 8):
                return None
            dr = krg - qrg + 7
            return np.where(inwin[None], rpb[:, dr][:, dci], np.float32(NEG))

        for i in range(8):
            blk = blockfor(11, 4 + i)
            tab[:, :, i * 64:(i + 1) * 64] = blk
        off = 512
        for s in (0, 1, 2, 3, 20, 21, 22):
            lo, hi = max(0, s - 7), min(15, s)
            for lr in range(lo, hi + 1):
                blk = blockfor(s, lr)
                if blk is not None:
                    tab[:, :, off:off + 64] = blk
                off += 64
        assert off == 1536
        tabs.append(np.ascontiguousarray(tab.reshape(8, 128, 1536)))
    return tabs


_PROGS = {}


def _prog(name):
    if name not in _PROGS:
        if name == "mixer":
            _PROGS[name] = build_mixer()
        elif name == "ffn":
            _PROGS[name] = build_ffn(False)
        else:
            _PROGS[name] = build_ffn(True)
    return _PROGS[name]


def _run(nc, in_maps):
    res = run_bass_kernel_spmd(nc, in_maps, core_ids=list(range(NCORES)))
    return [r["out"] for r in res.results]


def _to_XT(outs):
    return np.concatenate([o.transpose(1, 0, 2).reshape(D, T) for o in outs], axis=1)


def kernel(x, norm_mix, w_in, gm_ln_g, gm_ln_b, gm_w_s, gm_b_s, na_rpb, w_branch, w_out,
           norm_ffn, w_up, conv_k, conv_b, w_down, norm_final):
    f32 = np.float32
    x = np.asarray(x, f32)
    XT = np.ascontiguousarray(x[0].T)
    ident = np.eye(128, dtype=f32)
    depth = w_in.shape[0]
    for l in range(depth):
        wi = np.asarray(w_in[l], f32)
        win_t = _tile_w(wi)
        wv = wi[:, 1024:2048]
        wv_t = np.ascontiguousarray(wv.reshape(4, 4, 128, 2, 512).transpose(3, 0, 2, 1, 4)).reshape(2, 4, 128, 2048)
        wbr = np.asarray(w_branch[l], f32)
        wb_t = _tile_w(wbr)
        wout_t = _tile_w(np.asarray(w_out[l], f32))
        lng = np.ascontiguousarray(np.broadcast_to(np.asarray(gm_ln_g[l], f32)[None, :], (128, 1024)))
        lnb = np.ascontiguousarray(np.broadcast_to(np.asarray(gm_ln_b[l], f32)[None, :], (128, 1024)))
        ws = np.asarray(gm_w_s[l], f32)
        wsT = np.ascontiguousarray(ws.transpose(2, 0, 1)).reshape(128, 1024)
        bsr = np.ascontiguousarray(np.broadcast_to(np.asarray(gm_b_s[l], f32).reshape(1, 1024), (128, 1024)))
        btabs = _bias_tables(np.asarray(na_rpb[l], f32))
        gmix = _fm(np.asarray(norm_mix[l], f32))
        XT3 = XT.reshape(NFC, 128, 128, GW)
        in_maps = []
        for c in range(NCORES):
            rows = _slot_rows(c)
            xTc = np.ascontiguousarray(XT3[:, :, rows, :].transpose(1, 0, 2, 3)).reshape(128, NFC, EXT)
            in_maps.append({"xT": xTc, "gmix": gmix, "win_t": win_t, "wv_t": wv_t, "lng": lng, "lnb": lnb,
                            "wsT": wsT, "bs": bsr, "bias": btabs[c], "ident": ident, "wb_t": wb_t,
                            "wout_t": wout_t})
        XT = _to_XT(_run(_prog("mixer"), in_maps))
        final = (l == depth - 1)
        wup_t = _tile_w(np.asarray(w_up[l], f32))
        ckl = np.asarray(conv_k[l], f32)
        ck = np.ascontiguousarray(ckl.reshape(3, 2 * NDC, 128).transpose(2, 1, 0)).reshape(128, 2 * NDC * 3)
        cb = np.ascontiguousarray(np.asarray(conv_b[l], f32).reshape(2 * NDC, 128).T)
        wd = np.asarray(w_down[l], f32)
        wdp = np.zeros((4 * BLK_CH * 128, D), f32)
        wdp[:DFF] = wd
        wd_t = np.ascontiguousarray(wdp.reshape(4, BLK_CH, 128, NFC, 128).transpose(0, 3, 2, 1, 4)).reshape(4, NFC, 128, BLK_CH * 128)
        gffn = _fm(np.asarray(norm_ffn[l], f32))
        gfin = _fm(np.asarray(norm_final, f32))
        XTp = np.zeros((D, NCORES * T + 2), f32)
        XTp[:, 1:-1] = XT
        XTp3 = XTp.reshape(NFC, 128, NCORES * T + 2)
        in_maps = []
        for c in range(NCORES):
            xm = np.ascontiguousarray(XTp3[:, :, c * T:c * T + FEXT].transpose(1, 0, 2))
            in_maps.append({"xm": xm, "gffn": gffn, "wup_t": wup_t, "ck": ck, "cb": cb, "wd_t": wd_t, "gfin": gfin})
        XT = _to_XT(_run(_prog("ffn_final" if final else "ffn"), in_maps))
    return np.ascontiguousarray(XT.T)[None].astype(f32)
```
